# Optimizing a Trainium2 kernel written in Bass

```python
import jax, jax.numpy as jnp
from jax import lax
import numpy as np

D_MODEL = 2048
BATCH = 8
SEQ = 2048
DEPTH = 1

N_META = 16
POOL_WIDTH = D_MODEL // 2
POOL_WINDOWS = (2, 4, 8, 16)
N_POOL_GROUPS = len(POOL_WINDOWS)
POOL_GROUP_DIM = POOL_WIDTH // N_POOL_GROUPS
CONV_WIDTH = D_MODEL // 2
CONV_KERNEL = 31
D_FF = 4 * D_MODEL
IN_COLS = POOL_WIDTH + 2 * CONV_WIDTH + 2 * D_MODEL
RMS_EPS = 1e-6
LN_EPS = 1e-5

kernel_name = "hybrid_pool_conformer_gated_block"


def rms_norm(x, g):
    xf = x.astype(jnp.float32)
    y = xf * lax.rsqrt(jnp.mean(xf * xf, axis=-1, keepdims=True) + RMS_EPS)
    return (y * g.astype(jnp.float32)).astype(x.dtype)


def layer_norm(x, g, b):
    xf = x.astype(jnp.float32)
    mu = jnp.mean(xf, axis=-1, keepdims=True)
    var = jnp.mean(jnp.square(xf - mu), axis=-1, keepdims=True)
    y = (xf - mu) * lax.rsqrt(var + LN_EPS)
    return (y * g.astype(jnp.float32) + b.astype(jnp.float32)).astype(x.dtype)


def causal_multiscale_pool(z, w_grp, scale):
    B, L, _ = z.shape
    zf = z.astype(jnp.float32).reshape(B, L, N_POOL_GROUPS, POOL_GROUP_DIM)
    cs = jnp.cumsum(zf, axis=1)
    pos = jnp.arange(L)
    means = []
    for g, w in enumerate(POOL_WINDOWS):
        csg = cs[:, :, g]
        lag = jnp.pad(csg[:, : L - w], ((0, 0), (w, 0), (0, 0)))
        cnt = jnp.minimum(pos + 1, w).astype(jnp.float32)[None, :, None]
        means.append((csg - lag) / cnt)
    pooled = jnp.stack(means, axis=2)
    d = (pooled - zf).astype(z.dtype)
    y = jnp.einsum('blgc,gcd->blgd', d, w_grp).reshape(B, L, POOL_WIDTH)
    return y * scale


def conformer_conv(v, gate, w_dw, b_dw, ln_g, ln_b):
    a = v * jax.nn.sigmoid(gate)
    c = lax.conv_general_dilated(
        a, w_dw[:, None, :], window_strides=(1,),
        padding=[(CONV_KERNEL - 1, 0)],
        dimension_numbers=('NWC', 'WIO', 'NWC'),
        feature_group_count=CONV_WIDTH) + b_dw
    return jax.nn.silu(layer_norm(c, ln_g, ln_b))


def setup_inputs(seed: int = 0) -> dict:
    key = jax.random.key(seed)
    ks = jax.random.split(key, 24)
    f32 = jnp.float32
    n = lambda k, shape, s: jax.random.normal(k, shape, f32) * s
    gain = lambda k, shape: 1.0 + 0.05 * jax.random.normal(k, shape, f32)
    return {
        "x": jax.random.normal(ks[0], (BATCH, SEQ, D_MODEL), f32),
        "meta": n(ks[1], (N_META, D_MODEL), 1.0),
        "g_pre_mix": gain(ks[2], (DEPTH, D_MODEL)),
        "w_in": n(ks[3], (DEPTH, D_MODEL, IN_COLS), D_MODEL ** -0.5),
        "w_pool_grp": n(ks[4], (DEPTH, N_POOL_GROUPS, POOL_GROUP_DIM, POOL_GROUP_DIM), POOL_GROUP_DIM ** -0.5),
        "pool_scale": gain(ks[5], (DEPTH, POOL_WIDTH)),
        "w_pool_out": n(ks[6], (DEPTH, POOL_WIDTH, D_MODEL), POOL_WIDTH ** -0.5),
        "w_dw": n(ks[7], (DEPTH, CONV_KERNEL, CONV_WIDTH), CONV_KERNEL ** -0.5),
        "b_dw": n(ks[8], (DEPTH, CONV_WIDTH), 0.02),
        "conv_ln_g": gain(ks[9], (DEPTH, CONV_WIDTH)),
        "conv_ln_b": n(ks[10], (DEPTH, CONV_WIDTH), 0.02),
        "w_conv_out": n(ks[11], (DEPTH, CONV_WIDTH, D_MODEL), CONV_WIDTH ** -0.5),
        "w_o": n(ks[12], (DEPTH, D_MODEL, D_MODEL), D_MODEL ** -0.5),
        "g_post_mix": gain(ks[13], (DEPTH, D_MODEL)),
        "g_pre_mlp": gain(ks[14], (DEPTH, D_MODEL)),
        "w_up": n(ks[15], (DEPTH, D_MODEL, D_FF), D_MODEL ** -0.5),
        "w_down": n(ks[16], (DEPTH, D_FF, D_MODEL), D_FF ** -0.5),
        "g_post_mlp": gain(ks[17], (DEPTH, D_MODEL)),
    }


def reference(x, meta, g_pre_mix, w_in, w_pool_grp, pool_scale, w_pool_out,
              w_dw, b_dw, conv_ln_g, conv_ln_b, w_conv_out, w_o, g_post_mix,
              g_pre_mlp, w_up, w_down, g_post_mlp):
    B = x.shape[0]
    meta_b = jnp.broadcast_to(meta[None].astype(x.dtype), (B, N_META, D_MODEL))
    h = jnp.concatenate([meta_b, x], axis=1)
    splits = np.cumsum([POOL_WIDTH, CONV_WIDTH, CONV_WIDTH, D_MODEL]).tolist()
    for l in range(DEPTH):
        u = rms_norm(h, g_pre_mix[l])
        proj = u @ w_in[l]
        z_pool, v_conv, g_conv, gate_a, gate_b = jnp.split(proj, splits, axis=-1)
        y_a = causal_multiscale_pool(z_pool, w_pool_grp[l], pool_scale[l]) @ w_pool_out[l]
        y_b = conformer_conv(v_conv, g_conv, w_dw[l], b_dw[l],
                             conv_ln_g[l], conv_ln_b[l]) @ w_conv_out[l]
        m = jax.nn.sigmoid(gate_a) * y_a + jax.nn.sigmoid(gate_b) * y_b
        h = h + rms_norm(m @ w_o[l], g_post_mix[l])
        u = rms_norm(h, g_pre_mlp[l])
        f = jnp.square(jax.nn.relu(u @ w_up[l])) @ w_down[l]
        h = h + rms_norm(f, g_post_mlp[l])
    return h[:, N_META:]
```

```python
import numpy as np
import concourse.bass as bass
import concourse.mybir as mybir
from concourse.bass_utils import run_bass_kernel_spmd

F32 = mybir.dt.float32
BF16 = mybir.dt.bfloat16
AF = mybir.ActivationFunctionType
ALU = mybir.AluOpType
AX = mybir.AxisListType

D = 2048
SEQ = 2048
NMETA = 16
PW = 1024
CW = 1024
KCONV = 31
DFF = 8192
INC = 7168
TB = 512
NBLK = SEQ // TB
RMS_EPS = 1e-6
LN_EPS = 1e-5
G = 512
SB_BASE = 16896


class Tile:
    __slots__ = ("h", "keys", "name", "apf")

    def __init__(self, h, keys, name, apf=None):
        self.h = h
        self.keys = keys
        self.name = name
        self.apf = apf

    def ap(self):
        return self.apf() if self.apf is not None else self.h.ap()


class Op:
    __slots__ = ("eng", "fn", "deps", "signals", "sigval", "dma_sem", "idx")


class Sched:
    def __init__(self, nc):
        self.nc = nc
        self.ops = []
        self.queues = {"pe": [], "act": [], "dve": [], "pool": [], "sp": []}
        self.last_w = {}
        self.readers = {}
        self.dma_cnt = {}

    def add(self, eng, fn, reads=(), writes=(), dma_sem=None, safe=False):
        op = Op()
        op.eng = eng
        op.fn = fn
        op.signals = False
        op.sigval = None
        op.dma_sem = dma_sem
        op.idx = len(self.ops)
        deps = {}
        for t in reads:
            for k in t.keys:
                w = self.last_w.get(k)
                if w is not None:
                    deps[w] = True
                if k[0] == "P":
                    for r in self.readers.get(k, ()):
                        if r.eng != eng and r not in deps:
                            deps[r] = False
        for t in writes:
            for k in t.keys:
                w = self.last_w.get(k)
                if w is not None and w not in deps:
                    deps[w] = False
                for r in self.readers.get(k, ()):
                    if r not in deps:
                        deps[r] = False
        keep = []
        for d, raw in deps.items():
            if d is op:
                continue
            if d.dma_sem is None and dma_sem is None and d.eng == eng:
                if eng == "pe":
                    continue
            keep.append(d)
            if d.dma_sem is None:
                d.signals = True
        op.deps = keep
        for t in reads:
            for k in t.keys:
                self.readers.setdefault(k, []).append(op)
        for t in writes:
            for k in t.keys:
                self.last_w[k] = op
                self.readers[k] = []
        if dma_sem is not None:
            c = self.dma_cnt.get(dma_sem, 0) + 1
            self.dma_cnt[dma_sem] = c
            op.sigval = 16 * c
        self.ops.append(op)
        self.queues[eng].append(op)
        return op

    def emit(self, block, esems, final_waits):
        for q, ops in self.queues.items():
            c = 0
            for op in ops:
                if op.dma_sem is None and op.signals:
                    c += 1
                    op.sigval = c

        def run(q, e):
            waited = {}
            for op in self.queues[q]:
                need = {}
                for d in op.deps:
                    sem = d.dma_sem if d.dma_sem is not None else esems[d.eng]
                    v = d.sigval
                    if waited.get(sem, 0) < v and need.get(sem, 0) < v:
                        need[sem] = v
                for sem, v in need.items():
                    e.wait_ge(sem, v)
                    waited[sem] = v
                inst = op.fn(e)
                if op.dma_sem is not None:
                    inst.then_inc(op.dma_sem, 16)
                elif op.signals:
                    inst.then_inc(esems[q], 1)
            if q == "sp":
                for sem in final_waits:
                    e.wait_ge(sem, 16 * self.dma_cnt[sem])

        block.tensor(lambda e: run("pe", e))
        block.scalar(lambda e: run("act", e))
        block.vector(lambda e: run("dve", e))
        block.gpsimd(lambda e: run("pool", e))
        block.sync(lambda e: run("sp", e))


def build_program(nblk=NBLK):
    nc = bass.Bass("TRN2", target_bir_lowering=False)
    dr = {}

    def din(name, shape):
        dr[name] = nc.dram_tensor(name, list(shape), F32, kind="ExternalInput").ap()
        return dr[name]

    x_d = din("x", [SEQ, D])
    meta_d = din("meta", [NMETA, D])
    w_in_d = din("w_in", [D, INC])
    w_grp_d = din("w_grp", [4, 256, 256])
    w_po_d = din("w_po", [PW, D])
    w_co_d = din("w_co", [CW, D])
    w_o_d = din("w_o", [D, D])
    w_up_d = din("w_up", [D, DFF])
    w_dn_d = din("w_dn", [DFF, D])
    gpm_d = din("gpm", [128, 16])
    gpl_d = din("gpl", [128, 16])
    gqm_d = din("gqm", [1, D])
    gql_d = din("gql", [1, D])
    psc_d = din("psc", [128, 8])
    wdw_d = din("wdw", [128, 8 * KCONV])
    bdw_d = din("bdw", [128, 8])
    lng_d = din("lng", [128, 8])
    lnb_d = din("lnb", [128, 8])
    idn_d = din("idn", [128, 128])
    out_d = nc.dram_tensor("out", [SEQ, D], F32, kind="ExternalOutput").ap()

    S = Sched(nc)
    cnt = [0]

    def sb(off, shape, dt, name):
        nbytes = int(np.prod(shape[1:])) * (2 if dt == BF16 else 4)
        cnt[0] += 1
        h = nc.alloc_sbuf_tensor_at(f"{name}_{cnt[0]}", list(shape), dt, offset=off)
        keys = [("S", g) for g in range(off // G, (off + nbytes + G - 1) // G)]
        assert off + nbytes <= 229376, (name, off, nbytes)
        return Tile(h, keys, name)

    ptr = [SB_BASE]

    def bump(shape, dt, name):
        nbytes = int(np.prod(shape[1:])) * (2 if dt == BF16 else 4)
        t = sb(ptr[0], shape, dt, name)
        ptr[0] += (nbytes + G - 1) // G * G
        return t

    ident_f = bump([128, 128], F32, "identf")
    ident_b = bump([128, 128], BF16, "identb")
    ones_f = bump([128, 128], F32, "onesf")
    wdw = bump([128, 8 * KCONV], F32, "wdw")
    gpm = bump([128, 16], F32, "gpm")
    gpl = bump([128, 16], F32, "gpl")
    psc = bump([128, 8], F32, "psc")
    bdw = bump([128, 8], F32, "bdw")
    lng = bump([128, 8], F32, "lng")
    lnb = bump([128, 8], F32, "lnb")
    halo_z = bump([128, 8, 15], F32, "haloz")
    halo_a = bump([128, 8, 30], BF16, "haloa")
    uTm = bump([128, 16, 16], BF16, "uTm")
    vm = bump([128, 8, NMETA], F32, "vm")
    wgrp = bump([128, 4, 2, 256], BF16, "wgrp")
    NST = 12
    sts = [bump([128, 16], F32, f"st{i}") for i in range(NST)]
    gbuf = bump([128, D], F32, "gbuf")
    ring = [bump([128, 8, 512], BF16, f"ring{i}") for i in range(4)]
    HT0 = ptr[0]
    hT = [bump([128, D], F32, f"hT{i}") for i in range(4)]
    gsa = [sb(HT0 + i * 1024, [128, TB], BF16, f"gsa{i}") for i in range(16)]
    gsb = [sb(HT0 + 16384 + i * 1024, [128, TB], BF16, f"gsb{i}") for i in range(16)]
    uT_all = bump([128, 16, TB], BF16, "uTall")
    uT = [Tile(uT_all.h, uT_all.keys[2 * i:2 * i + 2], f"uT{i}", apf=(lambda i=i: uT_all.h.ap()[:, i, :]))
          for i in range(16)]
    C0 = ptr[0]
    hid = [sb(C0 + i * 1024, [128, TB], BF16, f"hid{i}") for i in range(64)]
    rs = [sb(C0 + 65536 + i * 2048, [128, TB], F32, f"rs{i}") for i in range(2)]
    fT = [sb(C0 + 69632 + i * 8192, [128, D], F32, f"fT{i}") for i in range(4)]
    assert C0 + 102400 <= 229376, C0
    xs = [sb(C0 + 69632 + i * 4096, [128, D], BF16, f"xs{i}") for i in range(2)]
    xin = [sb(C0 + 77824 + i * 8192, [128, D], F32, f"xin{i}") for i in range(2)]
    pA = [sb(C0 + 24576 + i * 1024, [128, TB], BF16, f"pA{i}") for i in range(8)]
    pB = [sb(C0 + 32768 + i * 1024, [128, TB], BF16, f"pB{i}") for i in range(8)]
    M0 = C0 + 40960
    zx = [sb(M0 + i * 2560, [128, 527], F32, f"zx{i}") for i in range(2)]
    ax = [sb(M0 + 5120 + i * 1536, [128, 542], BF16, f"ax{i}") for i in range(2)]
    dg = [sb(C0 + 8192 + i * 8192, [128, KCONV, 128], BF16, f"dg{i}") for i in range(2)]
    pt = [sb(M0 + 10240 + i * 2560, [128, 527], F32, f"pt{i}") for i in range(2)]
    dsb = [sb(M0 + 15360 + i * 1024, [128, TB], BF16, f"dsb{i}") for i in range(8)]
    sg = [sb(M0 + 23552 + i * 2048, [128, TB], F32, f"sg{i}") for i in range(2)]
    c_all = [sb(M0 + 27648 + i * 2048, [128, TB], F32, f"call{i}") for i in range(8)]
    csq = [sb(M0 + 44032 + i * 2048, [128, TB], F32, f"csq{i}") for i in range(2)]
    lnt = [sb(M0 + 48128 + i * 2048, [128, TB], F32, f"lnt{i}") for i in range(6)]
    assert M0 + 48128 + 6 * 2048 <= C0 + 102400
    mT = [sb(M0 + i * 1024, [128, TB], BF16, f"mT{i}") for i in range(16)]
    sa = [sb(M0 + 16384 + i * 2048, [128, TB], F32, f"sa{i}") for i in range(4)]
    sbt = [sb(M0 + 24576 + i * 2048, [128, TB], F32, f"sbt{i}") for i in range(4)]

    banks = []
    for i in range(8):
        h = nc.alloc_psum_tensor(f"pb{i}", [128, 512], F32)
        banks.append(Tile(h, [("P", i)], f"bank{i}"))
    bank_rr = [0]

    bank_reserved = set()

    def nb(n):
        r = []
        while len(r) < n:
            bk = banks[bank_rr[0] % 8]
            bank_rr[0] = (bank_rr[0] + 1) % 8
            if bk in bank_reserved:
                continue
            r.append(bk)
        return r

    sem_names = ["pe", "act", "dve", "pool", "r0", "r1", "r2", "r3", "x0", "x1",
                 "o0", "o1", "o2", "o3", "gb", "cst", "wg"]
    import contextlib
    with contextlib.ExitStack() as es:
        sems = {n: es.enter_context(nc.semaphore("s_" + n)) for n in sem_names}
        esems = {"pe": sems["pe"], "act": sems["act"], "dve": sems["dve"], "pool": sems["pool"]}
        ring_sems = [sems["r0"], sems["r1"], sems["r2"], sems["r3"]]
        xin_sems = [sems["x0"], sems["x1"]]
        out_sems = [sems["o0"], sems["o1"], sems["o2"], sems["o3"]]
        block = es.enter_context(nc.Block())

        flip = [0]

        def ev_engine():
            flip[0] ^= 1
            return "act" if flip[0] else "dve"

        def dma_in(q, dst_tile, dst_ap, src_ap, sem):
            S.add(q, lambda e: e.dma_start(out=dst_ap, in_=src_ap), reads=(), writes=(dst_tile,), dma_sem=sem)

        st_rr = [0]

        def st_tile():
            st_rr[0] = (st_rr[0] + 1) % NST
            return sts[st_rr[0]]

        ring_rr = [0]

        def load_w(src2d, r0, nk, c0, ncols=512):
            i = ring_rr[0]
            ring_rr[0] = (i + 1) % 4
            slot = ring[i]
            src = src2d[r0:r0 + nk * 128, c0:c0 + ncols].rearrange("(kc p) n -> p kc n", p=128)
            dst = slot.ap()[:, 0:nk, 0:ncols]
            S.add("pool", lambda e: e.dma_start(out=dst, in_=src), writes=(slot,), dma_sem=ring_sems[i])
            return slot

        def copy_scaled(dst_tile, dst_ap, src_tile, src_ap, scale_ap=None, eng=None, extra_reads=()):
            eng = eng or ev_engine()
            if eng == "act":
                if scale_ap is None:
                    f = lambda e: e.activation(out=dst_ap, in_=src_ap, func=AF.Copy)
                else:
                    f = lambda e: e.activation(out=dst_ap, in_=src_ap, func=AF.Copy, scale=scale_ap)
            else:
                if scale_ap is None:
                    f = lambda e: e.tensor_copy(out=dst_ap, in_=src_ap)
                else:
                    f = lambda e: e.tensor_scalar(out=dst_ap, in0=src_ap, scalar1=scale_ap, scalar2=None,
                                                  op0=ALU.mult)
            S.add(eng, f, reads=(src_tile,) + tuple(extra_reads), writes=(dst_tile,))

        def rstd_from_ss(ss_tile, ss_ap, ncols, inv_n, eps, rows=128):
            st = st_tile()
            a = st.ap()[0:rows, :]
            if ncols > 1:
                S.add("dve", lambda e: e.tensor_reduce(out=a[:, 0:1], in_=ss_ap, axis=AX.X, op=ALU.add),
                      reads=(ss_tile,), writes=(st,))
                S.add("dve", lambda e: e.tensor_scalar(out=a[:, 1:2], in0=a[:, 0:1], scalar1=inv_n, scalar2=eps,
                                                       op0=ALU.mult, op1=ALU.add), reads=(st,), writes=(st,))
            else:
                S.add("dve", lambda e: e.tensor_scalar(out=a[:, 1:2], in0=ss_ap, scalar1=inv_n, scalar2=eps,
                                                       op0=ALU.mult, op1=ALU.add), reads=(ss_tile,), writes=(st,))
            S.add("act", lambda e: e.activation(out=a[:, 2:3], in_=a[:, 1:2], func=AF.Sqrt),
                  reads=(st,), writes=(st,))
            S.add("dve", lambda e: e.reciprocal(out=a[:, 3:4], in_=a[:, 2:3]), reads=(st,), writes=(st,))
            return st, a[:, 3:4]

        def norm_transpose(src_tiles, nrows, gain, dst_fn):
            for ti, (src_t, src_ap) in enumerate(src_tiles):
                xsb = xs[ti % 2]
                xsa = xsb.ap()[0:nrows, :]
                st = st_tile()
                ssa = st.ap()[0:nrows, 0:1]
                S.add("act", lambda e, xsa=xsa, src_ap=src_ap, ssa=ssa: e.activation(
                    out=xsa, in_=src_ap, func=AF.Square, accum_out=ssa), reads=(src_t,), writes=(xsb, st))
                rt, ra = rstd_from_ss(st, ssa, 1, 1.0 / D, RMS_EPS, rows=nrows)
                S.add("dve", lambda e, xsa=xsa, src_ap=src_ap, ra=ra: e.tensor_scalar(
                    out=xsa, in0=src_ap, scalar1=ra, scalar2=None, op0=ALU.mult), reads=(src_t, rt), writes=(xsb,))
                for half in range(2):
                    bk = nb(1)[0]
                    bview = bk.ap().bitcast(BF16)

                    def tp(e, half=half, bview=bview, xsb=xsb):
                        inst = None
                        for i in range(8):
                            kc = half * 8 + i
                            inst = e.transpose(out=bview[:, i * nrows:(i + 1) * nrows],
                                               in_=xsb.ap()[0:nrows, kc * 128:(kc + 1) * 128],
                                               identity=ident_b.ap()[0:nrows, 0:nrows])
                        return inst
                    S.add("pe", tp, reads=(xsb, ident_b), writes=(bk,))
                    evq = ev_engine()
                    for i in range(8):
                        kc = half * 8 + i
                        dt_, da = dst_fn(kc, ti)
                        copy_scaled(dt_, da, bk, bview[:, i * nrows:(i + 1) * nrows],
                                    scale_ap=gain.ap()[:, kc:kc + 1], eng=evq, extra_reads=(gain,))

        csem = sems["cst"]
        n0 = len(S.ops)
        for t, d in ((ident_f, idn_d), (wdw, wdw_d), (gpm, gpm_d), (gpl, gpl_d), (psc, psc_d), (bdw, bdw_d),
                     (lng, lng_d), (lnb, lnb_d)):
            dma_in("sp", t, t.ap(), d, csem)
        for o in S.ops[n0:]:
            o.sigval = 16 * (len(S.ops) - n0)
        S.add("pool", lambda e: e.dma_start(out=wgrp.ap(),
                                            in_=w_grp_d.rearrange("g (kc p) d -> p g kc d", p=128)),
              writes=(wgrp,), dma_sem=sems["wg"])
        S.add("dve", lambda e: e.tensor_copy(out=ident_b.ap(), in_=ident_f.ap()), reads=(ident_f,), writes=(ident_b,))
        S.add("dve", lambda e: e.memset(ones_f.ap(), 1.0), writes=(ones_f,))
        S.add("dve", lambda e: e.memset(halo_a.ap(), 0.0), writes=(halo_a,))

        def proj_fm(wsrc, c0, kchunks, rhs_fn, ntok, bks, meta_bank=None):
            nkh = kchunks // 8
            slots = []
            for kh in range(nkh):
                slot = load_w(wsrc, kh * 1024, 8, c0)
                slots.append(slot)
                for oc in range(4):
                    def mm(e, kh=kh, oc=oc, slot=slot):
                        inst = None
                        for k in range(8):
                            kk = kh * 8 + k
                            rt, ra = rhs_fn(kk)
                            inst = e.matmul(bks[oc].ap()[:, 0:ntok], lhsT=slot.ap()[:, k, oc * 128:(oc + 1) * 128],
                                            rhs=ra, start=(kk == 0), stop=(kk == kchunks - 1))
                        return inst
                    rd = [slot] + [rhs_fn(kh * 8 + k)[0] for k in range(8)]
                    S.add("pe", mm, reads=rd, writes=(bks[oc],))
            if meta_bank is not None:
                def mmeta(e):
                    inst = None
                    for oc in range(4):
                        for kk in range(kchunks):
                            inst = e.matmul(meta_bank.ap()[:, oc * NMETA:(oc + 1) * NMETA],
                                            lhsT=slots[kk // 8].ap()[:, kk % 8, oc * 128:(oc + 1) * 128],
                                            rhs=uTm.ap()[:, kk, :], start=(kk == 0), stop=(kk == kchunks - 1))
                    return inst
                S.add("pe", mmeta, reads=slots + [uTm], writes=(meta_bank,))

        def glu_into(j, bank, ntok, a_tile, a_ap):
            s = sg[j % 2]
            S.add("act", lambda e: e.activation(out=s.ap()[:, 0:ntok], in_=bank.ap()[:, 0:ntok], func=AF.Sigmoid),
                  reads=(bank,), writes=(s,))
            S.add("dve", lambda e: e.tensor_tensor(out=a_ap, in0=c_all[j].ap()[:, 0:ntok], in1=s.ap()[:, 0:ntok],
                                                   op=ALU.mult), reads=(c_all[j], s), writes=(a_tile,))

        dma_in("sp", xin[0], xin[0].ap()[0:NMETA, :], meta_d, xin_sems[0])
        norm_transpose([(xin[0], xin[0].ap()[0:NMETA, :])], NMETA, gpm,
                       lambda kc, ti: (uTm, uTm.ap()[:, kc, :]))
        def nt_front(src_t, src_ap, tt, scale_eng="dve"):
            xsb = xs[tt % 2]
            st = st_tile()
            ssa = st.ap()[:, 0:1]
            S.add("act", lambda e: e.activation(out=xsb.ap(), in_=src_ap, func=AF.Square, accum_out=ssa),
                  reads=(src_t,), writes=(xsb, st))
            rt, ra = rstd_from_ss(st, st.ap()[:, 0:1], 1, 1.0 / D, RMS_EPS)
            if scale_eng == "act":
                S.add("act", lambda e: e.activation(out=xsb.ap(), in_=src_ap, func=AF.Copy, scale=ra),
                      reads=(src_t, rt), writes=(xsb,))
            else:
                S.add("dve", lambda e: e.tensor_scalar(out=xsb.ap(), in0=src_ap, scalar1=ra, scalar2=None, op0=ALU.mult),
                      reads=(src_t, rt), writes=(xsb,))

        def nt_back(tt, gain):
            xsb = xs[tt % 2]
            for half in range(2):
                bk = nb(1)[0]
                bview = bk.ap().bitcast(BF16)

                def tp(e, half=half, bview=bview):
                    inst = None
                    for i in range(8):
                        kc = half * 8 + i
                        inst = e.transpose(out=bview[:, i * 128:(i + 1) * 128],
                                           in_=xsb.ap()[:, kc * 128:(kc + 1) * 128], identity=ident_b.ap())
                    return inst
                S.add("pe", tp, reads=(xsb, ident_b), writes=(bk,))
                o_ap = uT_all.ap()[:, half * 8:(half + 1) * 8, tt * 128:(tt + 1) * 128]
                i0 = bview[:, 0:1024].rearrange("p (k n) -> p k n", k=8)
                i1 = gain.ap()[:, half * 8:(half + 1) * 8].unsqueeze(2).broadcast_to([128, 8, 128])
                S.add("dve", lambda e, o_ap=o_ap, i0=i0, i1=i1: e.tensor_tensor(out=o_ap, in0=i0, in1=i1, op=ALU.mult),
                      reads=(bk, gain), writes=tuple(uT[half * 8 + i] for i in range(8)))

        def phase_A_front(b, tts):
            for tt in tts:
                xi = xin[tt % 2]
                dma_in("sp", xi, xi.ap(), x_d[b * TB + tt * 128:b * TB + (tt + 1) * 128, :], xin_sems[tt % 2])
                nt_front(xi, xi.ap(), tt)

        def phase_A_back(tts):
            for tt in tts:
                nt_back(tt, gpm)

        for b in range(nblk):
            t0 = b * TB
            if b == 0:
                phase_A_front(0, (0, 1))
                phase_A_back((0, 1))
                phase_A_front(0, (2, 3))
                phase_A_back((2, 3))
            u_rhs = lambda kk: (uT[kk], uT[kk].ap())

            def g_front(j, bk):
                a = ax[j % 2]
                aa = a.ap()
                glu_into(j, bk, TB, a, aa[:, 30:542])
                S.add("dve", lambda e, aa=aa, j=j: e.tensor_copy(out=aa[:, 0:30], in_=halo_a.ap()[:, j, :]),
                      reads=(halo_a,), writes=(a,))
                S.add("dve", lambda e, aa=aa, j=j: e.tensor_copy(out=halo_a.ap()[:, j, :], in_=aa[:, 512:542]),
                      reads=(a,), writes=(halo_a,))
                d = dg[j % 2]
                S.add("dve", lambda e, d=d, j=j: e.tensor_tensor(
                    out=d.ap(), in0=ident_b.ap().unsqueeze(1).broadcast_to([128, KCONV, 128]),
                    in1=wdw.ap()[:, j * KCONV:(j + 1) * KCONV].unsqueeze(2).broadcast_to([128, KCONV, 128]),
                    op=ALU.mult), reads=(ident_b, wdw), writes=(d,))

            def conv_pe(j):
                a = ax[j % 2]
                d = dg[j % 2]
                bk = nb(1)[0]

                def cm(e, a=a, d=d, bk=bk):
                    inst = None
                    for k in range(KCONV):
                        inst = e.matmul(bk.ap(), lhsT=d.ap()[:, k, :], rhs=a.ap()[:, k:k + 512],
                                        start=(k == 0), stop=(k == KCONV - 1))
                    return inst
                S.add("pe", cm, reads=(a, d), writes=(bk,))
                S.add("act", lambda e, j=j, bk=bk: e.activation(out=c_all[j].ap(), in_=bk.ap(), func=AF.Identity,
                                                                bias=bdw.ap()[:, j:j + 1]),
                      reads=(bk, bdw), writes=(c_all[j],))

            def gate_proj(og, which):
                c_base, gdst = ((3072, gsa), (5120, gsb))[which]
                bg = nb(4)
                proj_fm(w_in_d, c_base + og * 512, 16, u_rhs, TB, bg)
                for oc in range(4):
                    jj = og * 4 + oc
                    S.add("act", lambda e, oc=oc, bg=bg, gt=gdst[jj]: e.activation(
                        out=gt.ap(), in_=bg[oc].ap(), func=AF.Sigmoid), reads=(bg[oc],), writes=(gdst[jj],))

            for cg in range(4):
                bks = nb(4)
                mbk = nb(1)[0] if b == 0 else None
                proj_fm(w_in_d, cg * 512, 16, u_rhs, TB, bks, meta_bank=mbk)
                for oc in range(4):
                    j = (cg % 2) * 4 + oc
                    bk = bks[oc]
                    if b == 0:
                        if cg < 2:
                            copy_scaled(halo_z, halo_z.ap()[:, j, :], mbk, mbk.ap()[:, oc * NMETA + 1:(oc + 1) * NMETA])
                        else:
                            copy_scaled(vm, vm.ap()[:, j, :], mbk, mbk.ap()[:, oc * NMETA:(oc + 1) * NMETA])
                    if cg < 2:
                        z = zx[j % 2]
                        za = z.ap()
                        copy_scaled(z, za[:, 15:527], bk, bk.ap())
                        S.add("dve", lambda e, za=za, j=j: e.tensor_copy(out=za[:, 0:15], in_=halo_z.ap()[:, j, :]),
                              reads=(halo_z,), writes=(z,))
                        wlog = j // 2 + 1
                        src_t = z
                        for s_ in range(wlog):
                            sh = 1 << s_
                            lo = (1 << (s_ + 1)) - 1
                            dst_t = pt[s_ % 2]
                            S.add("dve", lambda e, d=dst_t.ap(), s=src_t.ap(), lo=lo, sh=sh: e.tensor_tensor(
                                out=d[:, lo:527], in0=s[:, lo:527], in1=s[:, lo - sh:527 - sh], op=ALU.add),
                                reads=(src_t,), writes=(dst_t,))
                            src_t = dst_t
                        dd = dsb[j]
                        S.add("dve", lambda e, dd=dd, s=src_t.ap(), za=za, w=float(1 << wlog): e.scalar_tensor_tensor(
                            out=dd.ap(), in0=s[:, 15:527], scalar=1.0 / w, in1=za[:, 15:527],
                            op0=ALU.mult, op1=ALU.subtract), reads=(src_t, z), writes=(dd,))
                        S.add("dve", lambda e, za=za, j=j: e.tensor_copy(out=halo_z.ap()[:, j, :], in_=za[:, 512:527]),
                              reads=(z,), writes=(halo_z,))
                    else:
                        copy_scaled(c_all[j], c_all[j].ap(), bk, bk.ap(), eng="act")
            gq = [(0, 0), (0, 1), (1, 0), (1, 1)]
            for half in range(2):
                bks = nb(4)
                mbk = nb(1)[0] if b == 0 else None
                proj_fm(w_in_d, (4 + half) * 512, 16, u_rhs, TB, bks, meta_bank=mbk)
                j0 = half * 4
                if b == 0:
                    for oc in range(4):
                        j = j0 + oc
                        sm = sg[j % 2]
                        S.add("act", lambda e, sm=sm, oc=oc, mbk=mbk: e.activation(
                            out=sm.ap()[:, 0:NMETA], in_=mbk.ap()[:, oc * NMETA:(oc + 1) * NMETA], func=AF.Sigmoid),
                            reads=(mbk,), writes=(sm,))
                        S.add("dve", lambda e, sm=sm, j=j: e.tensor_tensor(
                            out=halo_a.ap()[:, j, 14:30], in0=vm.ap()[:, j, :], in1=sm.ap()[:, 0:NMETA], op=ALU.mult),
                            reads=(vm, sm), writes=(halo_a,))
                bank_reserved.update(bks)
                g_front(j0, bks[0])
                bank_reserved.discard(bks[0])
                g_front(j0 + 1, bks[1])
                bank_reserved.discard(bks[1])
                gate_proj(*gq[half * 2])
                conv_pe(j0)
                conv_pe(j0 + 1)
                g_front(j0 + 2, bks[2])
                bank_reserved.discard(bks[2])
                g_front(j0 + 3, bks[3])
                bank_reserved.discard(bks[3])
                gate_proj(*gq[half * 2 + 1])
                conv_pe(j0 + 2)
                conv_pe(j0 + 3)
            gate_proj(2, 0)
            for gi in range(4):
                for dj in range(2):
                    gb_ = nb(1)[0]

                    def gm(e, gi=gi, dj=dj, gb_=gb_):
                        inst = None
                        for kc in range(2):
                            inst = e.matmul(gb_.ap(), lhsT=wgrp.ap()[:, gi, kc, dj * 128:(dj + 1) * 128],
                                            rhs=dsb[2 * gi + kc].ap(), start=(kc == 0), stop=(kc == 1))
                        return inst
                    S.add("pe", gm, reads=(wgrp, dsb[2 * gi], dsb[2 * gi + 1]), writes=(gb_,))
                    jj = 2 * gi + dj
                    copy_scaled(pA[jj], pA[jj].ap(), gb_, gb_.ap(), scale_ap=psc.ap()[:, jj:jj + 1],
                                extra_reads=(psc,))
            b_sum, b_sq = nb(2)
            for j in range(8):
                q = csq[j % 2]
                S.add("act", lambda e, q=q, j=j: e.activation(out=q.ap(), in_=c_all[j].ap(), func=AF.Square),
                      reads=(c_all[j],), writes=(q,))

                def lm(e, q=q, j=j, b_sum=b_sum, b_sq=b_sq):
                    e.matmul(b_sum.ap(), lhsT=ones_f.ap(), rhs=c_all[j].ap(), start=(j == 0), stop=(j == 7))
                    return e.matmul(b_sq.ap(), lhsT=ones_f.ap(), rhs=q.ap(), start=(j == 0), stop=(j == 7))
                S.add("pe", lm, reads=(ones_f, c_all[j], q), writes=(b_sum, b_sq))
            mean, var, sd, rstd, nmr = lnt[0], lnt[1], lnt[2], lnt[3], lnt[4]
            S.add("dve", lambda e, b_sum=b_sum: e.tensor_scalar(out=mean.ap(), in0=b_sum.ap(), scalar1=1.0 / CW, scalar2=None,
                                                   op0=ALU.mult), reads=(b_sum,), writes=(mean,))
            S.add("dve", lambda e: e.tensor_tensor(out=var.ap(), in0=mean.ap(), in1=mean.ap(), op=ALU.mult),
                  reads=(mean,), writes=(var,))
            S.add("dve", lambda e, b_sq=b_sq: e.scalar_tensor_tensor(out=var.ap(), in0=b_sq.ap(), scalar=1.0 / CW, in1=var.ap(),
                                                          op0=ALU.mult, op1=ALU.subtract),
                  reads=(b_sq, var), writes=(var,))
            S.add("dve", lambda e: e.tensor_scalar(out=var.ap(), in0=var.ap(), scalar1=LN_EPS, scalar2=None,
                                                   op0=ALU.add), reads=(var,), writes=(var,))
            gate_proj(2, 1)
            gate_proj(3, 0)
            gate_proj(3, 1)
            S.add("act", lambda e: e.activation(out=sd.ap(), in_=var.ap(), func=AF.Sqrt), reads=(var,), writes=(sd,))
            S.add("dve", lambda e: e.reciprocal(out=rstd.ap(), in_=sd.ap()), reads=(sd,), writes=(rstd,))
            S.add("dve", lambda e: e.scalar_tensor_tensor(out=nmr.ap(), in0=mean.ap(), scalar=-1.0, in1=rstd.ap(),
                                                          op0=ALU.mult, op1=ALU.mult),
                  reads=(mean, rstd), writes=(nmr,))
            for j in range(8):
                tmp = csq[j % 2]
                S.add("dve", lambda e, tmp=tmp, j=j: e.tensor_tensor(out=tmp.ap(), in0=c_all[j].ap(), in1=rstd.ap(),
                                                                     op=ALU.mult),
                      reads=(c_all[j], rstd), writes=(tmp,))
                S.add("dve", lambda e, tmp=tmp: e.tensor_tensor(out=tmp.ap(), in0=tmp.ap(), in1=nmr.ap(), op=ALU.add),
                      reads=(tmp, nmr), writes=(tmp,))
                S.add("act", lambda e, tmp=tmp, j=j: e.activation(out=pB[j].ap(), in_=tmp.ap(), func=AF.Silu,
                                                                  scale=lng.ap()[:, j:j + 1], bias=lnb.ap()[:, j:j + 1]),
                      reads=(tmp, lng, lnb), writes=(pB[j],))

            for og in range(4):
                bya = nb(4)
                proj_fm(w_po_d, og * 512, 8, lambda kk: (pA[kk], pA[kk].ap()), TB, bya)
                for oc in range(4):
                    jj = og * 4 + oc
                    S.add("dve", lambda e, oc=oc, bya=bya, jj=jj: e.tensor_tensor(out=sa[oc].ap(), in0=bya[oc].ap(),
                                                                                 in1=gsa[jj].ap(), op=ALU.mult),
                          reads=(bya[oc], gsa[jj]), writes=(sa[oc],))
                byb = nb(4)
                proj_fm(w_co_d, og * 512, 8, lambda kk: (pB[kk], pB[kk].ap()), TB, byb)
                for oc in range(4):
                    jj = og * 4 + oc
                    S.add("dve", lambda e, oc=oc, byb=byb, jj=jj: e.tensor_tensor(out=sbt[oc].ap(), in0=byb[oc].ap(),
                                                                                 in1=gsb[jj].ap(), op=ALU.mult),
                          reads=(byb[oc], gsb[jj]), writes=(sbt[oc],))
                    S.add("dve", lambda e, oc=oc, jj=jj: e.tensor_tensor(out=mT[jj].ap(), in0=sbt[oc].ap(),
                                                                         in1=sa[oc].ap(), op=ALU.add),
                          reads=(sbt[oc], sa[oc]), writes=(mT[jj],))

            def proj_tm(wsrc, nkh, act_tiles, dst, gsrc_d, gsem, hook=None):
                dma_in("sp", gbuf, gbuf.ap(), gsrc_d.partition_broadcast(128), gsem)
                pss = [st_tile() for _ in range(4)]
                for og in range(4):
                    bks = nb(4)
                    for kh in range(nkh):
                        slot = load_w(wsrc, kh * 1024, 8, og * 512)
                        for tt in range(4):
                            def mm(e, kh=kh, tt=tt, slot=slot, bks=bks):
                                inst = None
                                for k in range(8):
                                    kk = kh * 8 + k
                                    inst = e.matmul(bks[tt].ap(), lhsT=act_tiles[kk].ap()[:, tt * 128:(tt + 1) * 128],
                                                    rhs=slot.ap()[:, k, :], start=(kk == 0), stop=(kk == nkh * 8 - 1))
                                return inst
                            S.add("pe", mm, reads=[slot] + [act_tiles[kh * 8 + k] for k in range(8)],
                                  writes=(bks[tt],))
                    if og == 0 and hook is not None:
                        bank_reserved.update(bks)
                        hook()
                        bank_reserved.clear()
                    for tt in range(4):
                        S.add("dve", lambda e, tt=tt, og=og, bks=bks: e.tensor_copy(
                            out=dst[tt].ap()[:, og * 512:(og + 1) * 512], in_=bks[tt].ap()),
                            reads=(bks[tt],), writes=(dst[tt],))
                        q = csq[tt % 2] if dst is hT else rs[tt % 2]
                        S.add("act", lambda e, tt=tt, og=og, q=q: e.activation(
                            out=q.ap(), in_=dst[tt].ap()[:, og * 512:(og + 1) * 512], func=AF.Square,
                            accum_out=pss[tt].ap()[:, 4 + og:5 + og]),
                            reads=(dst[tt],), writes=(q, pss[tt]))
                return pss

            pss = proj_tm(w_o_d, 2, mT, hT, gqm_d, sems["gb"])
            for tt in range(4):
                rt, ra = rstd_from_ss(pss[tt], pss[tt].ap()[:, 4:8], 4, 1.0 / D, RMS_EPS)
                xi = xin[tt % 2]
                dma_in("sp", xi, xi.ap(), x_d[t0 + tt * 128:t0 + (tt + 1) * 128, :], xin_sems[tt % 2])
                h = hT[tt]
                S.add("dve", lambda e, h=h, ra=ra: e.scalar_tensor_tensor(out=h.ap(), in0=h.ap(), scalar=ra,
                                                                          in1=gbuf.ap(), op0=ALU.mult, op1=ALU.mult),
                      reads=(h, rt, gbuf), writes=(h,))
                S.add("pool", lambda e, h=h, xi=xi: e.tensor_tensor(out=h.ap(), in0=h.ap(), in1=xi.ap(), op=ALU.add),
                      reads=(h, xi), writes=(h,))
                if tt >= 1:
                    nt_back(tt - 1, gpl)
                nt_front(h, h.ap(), tt, scale_eng="act")
            nt_back(3, gpl)

            for cg in range(16):
                bks = nb(4)
                proj_fm(w_up_d, cg * 512, 16, u_rhs, TB, bks)
                for oc in range(4):
                    jj = cg * 4 + oc
                    r = rs[jj % 2]
                    S.add("act", lambda e, r=r, oc=oc, bks=bks: e.activation(out=r.ap(), in_=bks[oc].ap(), func=AF.Relu),
                          reads=(bks[oc],), writes=(r,))
                    S.add("dve", lambda e, r=r, jj=jj: e.tensor_tensor(out=hid[jj].ap(), in0=r.ap(), in1=r.ap(),
                                                                       op=ALU.mult),
                          reads=(r,), writes=(hid[jj],))

            hook = None
            if b + 1 < nblk:
                phase_A_front(b + 1, (0, 1))

                def hook(b=b):
                    phase_A_back((0, 1))
                    phase_A_front(b + 1, (2, 3))
                    phase_A_back((2, 3))

            pss = proj_tm(w_dn_d, 8, hid, fT, gql_d, sems["gb"], hook=hook)
            for tt in range(4):
                rt, ra = rstd_from_ss(pss[tt], pss[tt].ap()[:, 4:8], 4, 1.0 / D, RMS_EPS)
                f = fT[tt]
                S.add("dve", lambda e, f=f, ra=ra: e.scalar_tensor_tensor(out=f.ap(), in0=f.ap(), scalar=ra,
                                                                          in1=gbuf.ap(), op0=ALU.mult, op1=ALU.mult),
                      reads=(f, rt, gbuf), writes=(f,))
                S.add("pool", lambda e, f=f, tt=tt: e.tensor_tensor(out=f.ap(), in0=f.ap(), in1=hT[tt].ap(), op=ALU.add),
                      reads=(f, hT[tt]), writes=(f,))
                S.add("sp", lambda e, f=f, tt=tt, t0=t0: e.dma_start(
                    out=out_d[t0 + tt * 128:t0 + (tt + 1) * 128, :], in_=f.ap()),
                    reads=(f,), dma_sem=out_sems[tt])

        S.emit(block, esems, out_sems)
    return nc


_NC_CACHE = {}


def kernel(x, meta, g_pre_mix, w_in, w_pool_grp, pool_scale, w_pool_out, w_dw, b_dw, conv_ln_g, conv_ln_b,
           w_conv_out, w_o, g_post_mix, g_pre_mlp, w_up, w_down, g_post_mlp):
    f = lambda a: np.ascontiguousarray(np.asarray(a, dtype=np.float32))
    x = f(x)
    B = x.shape[0]

    def chan(v, n):
        return f(np.asarray(v, dtype=np.float32).reshape(n, 128).T)

    shared = {
        "meta": f(meta),
        "w_in": f(np.asarray(w_in)[0]),
        "w_grp": f(np.asarray(w_pool_grp)[0]),
        "w_po": f(np.asarray(w_pool_out)[0]),
        "w_co": f(np.asarray(w_conv_out)[0]),
        "w_o": f(np.asarray(w_o)[0]),
        "w_up": f(np.asarray(w_up)[0]),
        "w_dn": f(np.asarray(w_down)[0]),
        "gpm": chan(np.asarray(g_pre_mix)[0], 16),
        "gpl": chan(np.asarray(g_pre_mlp)[0], 16),
        "gqm": f(np.asarray(g_post_mix)[0].reshape(1, D)),
        "gql": f(np.asarray(g_post_mlp)[0].reshape(1, D)),
        "psc": chan(np.asarray(pool_scale)[0], 8),
        "wdw": f(np.asarray(w_dw, dtype=np.float32)[0].T.reshape(8, 128, KCONV).transpose(1, 0, 2).reshape(128, 8 * KCONV)),
        "bdw": chan(np.asarray(b_dw)[0], 8),
        "lng": chan(np.asarray(conv_ln_g)[0], 8),
        "lnb": chan(np.asarray(conv_ln_b)[0], 8),
        "idn": np.eye(128, dtype=np.float32),
    }
    if "nc" not in _NC_CACHE:
        _NC_CACHE["nc"] = build_program()
    nc = _NC_CACHE["nc"]
    in_maps = []
    for c in range(B):
        m = dict(shared)
        m["x"] = x[c]
        in_maps.append(m)
    res = run_bass_kernel_spmd(nc, in_maps, core_ids=list(range(B)))
    return np.stack([np.asarray(r["out"], dtype=np.float32) for r in res.results], axis=0)
```

```python
import numpy as np
import concourse.bass as bass
import concourse.mybir as mybir
from concourse.bass_utils import run_bass_kernel_spmd

F32 = mybir.dt.float32
BF16 = mybir.dt.bfloat16
AF = mybir.ActivationFunctionType
ALU = mybir.AluOpType
AX = mybir.AxisListType

D = 2048
SEQ = 2048
NMETA = 16
PW = 1024
CW = 1024
KCONV = 31
DFF = 8192
INC = 7168
TB = 512
NBLK = SEQ // TB
RMS_EPS = 1e-6
LN_EPS = 1e-5
G = 512
SB_BASE = 16896


class Tile:
    __slots__ = ("h", "keys", "name", "apf")

    def __init__(self, h, keys, name, apf=None):
        self.h = h
        self.keys = keys
        self.name = name
        self.apf = apf

    def ap(self):
        return self.apf() if self.apf is not None else self.h.ap()


class Op:
    __slots__ = ("eng", "fn", "deps", "signals", "sigval", "dma_sem", "idx")


class Sched:
    def __init__(self, nc):
        self.nc = nc
        self.ops = []
        self.queues = {"pe": [], "act": [], "dve": [], "pool": [], "sp": []}
        self.last_w = {}
        self.readers = {}
        self.dma_cnt = {}

    def add(self, eng, fn, reads=(), writes=(), dma_sem=None, safe=False):
        op = Op()
        op.eng = eng
        op.fn = fn
        op.signals = False
        op.sigval = None
        op.dma_sem = dma_sem
        op.idx = len(self.ops)
        deps = {}
        for t in reads:
            for k in t.keys:
                w = self.last_w.get(k)
                if w is not None:
                    deps[w] = True
                if k[0] == "P":
                    for r in self.readers.get(k, ()):
                        if r.eng != eng and r not in deps:
                            deps[r] = False
        for t in writes:
            for k in t.keys:
                w = self.last_w.get(k)
                if w is not None and w not in deps:
                    deps[w] = False
                for r in self.readers.get(k, ()):
                    if r not in deps:
                        deps[r] = False
        keep = []
        for d, raw in deps.items():
            if d is op:
                continue
            if d.dma_sem is None and dma_sem is None and d.eng == eng:
                if eng == "pe":
                    continue
            keep.append(d)
            if d.dma_sem is None:
                d.signals = True
        op.deps = keep
        for t in reads:
            for k in t.keys:
                self.readers.setdefault(k, []).append(op)
        for t in writes:
            for k in t.keys:
                self.last_w[k] = op
                self.readers[k] = []
        if dma_sem is not None:
            c = self.dma_cnt.get(dma_sem, 0) + 1
            self.dma_cnt[dma_sem] = c
            op.sigval = 16 * c
        self.ops.append(op)
        self.queues[eng].append(op)
        return op

    def emit(self, block, esems, final_waits):
        for q, ops in self.queues.items():
            c = 0
            for op in ops:
                if op.dma_sem is None and op.signals:
                    c += 1
                    op.sigval = c

        def run(q, e):
            waited = {}
            for op in self.queues[q]:
                need = {}
                for d in op.deps:
                    sem = d.dma_sem if d.dma_sem is not None else esems[d.eng]
                    v = d.sigval
                    if waited.get(sem, 0) < v and need.get(sem, 0) < v:
                        need[sem] = v
                for sem, v in need.items():
                    e.wait_ge(sem, v)
                    waited[sem] = v
                inst = op.fn(e)
                if op.dma_sem is not None:
                    inst.then_inc(op.dma_sem, 16)
                elif op.signals:
                    inst.then_inc(esems[q], 1)
            if q == "sp":
                for sem in final_waits:
                    e.wait_ge(sem, 16 * self.dma_cnt[sem])

        block.tensor(lambda e: run("pe", e))
        block.scalar(lambda e: run("act", e))
        block.vector(lambda e: run("dve", e))
        block.gpsimd(lambda e: run("pool", e))
        block.sync(lambda e: run("sp", e))


def build_program(nblk=NBLK):
    nc = bass.Bass("TRN2", target_bir_lowering=False)
    dr = {}

    def din(name, shape):
        dr[name] = nc.dram_tensor(name, list(shape), F32, kind="ExternalInput").ap()
        return dr[name]

    x_d = din("x", [SEQ, D])
    meta_d = din("meta", [NMETA, D])
    w_in_d = din("w_in", [D, INC])
    w_grp_d = din("w_grp", [4, 256, 256])
    w_po_d = din("w_po", [PW, D])
    w_co_d = din("w_co", [CW, D])
    w_o_d = din("w_o", [D, D])
    w_up_d = din("w_up", [D, DFF])
    w_dn_d = din("w_dn", [DFF, D])
    gpm_d = din("gpm", [128, 16])
    gpl_d = din("gpl", [128, 16])
    gqm_d = din("gqm", [1, D])
    gql_d = din("gql", [1, D])
    psc_d = din("psc", [128, 8])
    wdw_d = din("wdw", [128, 8 * KCONV])
    bdw_d = din("bdw", [128, 8])
    lng_d = din("lng", [128, 8])
    lnb_d = din("lnb", [128, 8])
    idn_d = din("idn", [128, 128])
    out_d = nc.dram_tensor("out", [SEQ, D], F32, kind="ExternalOutput").ap()

    S = Sched(nc)
    cnt = [0]

    def sb(off, shape, dt, name):
        nbytes = int(np.prod(shape[1:])) * (2 if dt == BF16 else 4)
        cnt[0] += 1
        h = nc.alloc_sbuf_tensor_at(f"{name}_{cnt[0]}", list(shape), dt, offset=off)
        keys = [("S", g) for g in range(off // G, (off + nbytes + G - 1) // G)]
        assert off + nbytes <= 229376, (name, off, nbytes)
        return Tile(h, keys, name)

    ptr = [SB_BASE]

    def bump(shape, dt, name):
        nbytes = int(np.prod(shape[1:])) * (2 if dt == BF16 else 4)
        t = sb(ptr[0], shape, dt, name)
        ptr[0] += (nbytes + G - 1) // G * G
        return t

    ident_f = bump([128, 128], F32, "identf")
    ident_b = bump([128, 128], BF16, "identb")
    ones_f = bump([128, 128], F32, "onesf")
    wdw = bump([128, 8 * KCONV], F32, "wdw")
    gpm = bump([128, 16], F32, "gpm")
    gpl = bump([128, 16], F32, "gpl")
    psc = bump([128, 8], F32, "psc")
    bdw = bump([128, 8], F32, "bdw")
    lng = bump([128, 8], F32, "lng")
    lnb = bump([128, 8], F32, "lnb")
    halo_z = bump([128, 8, 15], F32, "haloz")
    halo_a = bump([128, 8, 30], BF16, "haloa")
    uTm = bump([128, 16, 16], BF16, "uTm")
    vm = bump([128, 8, NMETA], F32, "vm")
    wgrp = bump([128, 4, 2, 256], BF16, "wgrp")
    NST = 12
    sts = [bump([128, 16], F32, f"st{i}") for i in range(NST)]
    gbuf = bump([128, D], F32, "gbuf")
    ring = [bump([128, 8, 512], BF16, f"ring{i}") for i in range(4)]
    HT0 = ptr[0]
    hT = [bump([128, D], F32, f"hT{i}") for i in range(4)]
    gsa = [sb(HT0 + i * 1024, [128, TB], BF16, f"gsa{i}") for i in range(16)]
    gsb = [sb(HT0 + 16384 + i * 1024, [128, TB], BF16, f"gsb{i}") for i in range(16)]
    uT_all = bump([128, 16, TB], BF16, "uTall")
    uT = [Tile(uT_all.h, uT_all.keys[2 * i:2 * i + 2], f"uT{i}", apf=(lambda i=i: uT_all.h.ap()[:, i, :]))
          for i in range(16)]
    C0 = ptr[0]
    hid = [sb(C0 + i * 1024, [128, TB], BF16, f"hid{i}") for i in range(64)]
    rs = [sb(C0 + 65536 + i * 2048, [128, TB], F32, f"rs{i}") for i in range(2)]
    fT = [sb(C0 + 69632 + i * 8192, [128, D], F32, f"fT{i}") for i in range(4)]
    assert C0 + 102400 <= 229376, C0
    xs = [sb(C0 + 69632 + i * 4096, [128, D], BF16, f"xs{i}") for i in range(2)]
    xin = [sb(C0 + 77824 + i * 8192, [128, D], F32, f"xin{i}") for i in range(2)]
    pA = [sb(C0 + 24576 + i * 1024, [128, TB], BF16, f"pA{i}") for i in range(8)]
    pB = [sb(C0 + 32768 + i * 1024, [128, TB], BF16, f"pB{i}") for i in range(8)]
    M0 = C0 + 40960
    zx = [sb(M0 + i * 2560, [128, 527], F32, f"zx{i}") for i in range(2)]
    ax = [sb(M0 + 5120 + i * 1536, [128, 542], BF16, f"ax{i}") for i in range(2)]
    dg = [sb(C0 + 8192 + i * 8192, [128, KCONV, 128], BF16, f"dg{i}") for i in range(2)]
    pt = [sb(M0 + 10240 + i * 2560, [128, 527], F32, f"pt{i}") for i in range(2)]
    dsb = [sb(M0 + 15360 + i * 1024, [128, TB], BF16, f"dsb{i}") for i in range(8)]
    sg = [sb(M0 + 23552 + i * 2048, [128, TB], F32, f"sg{i}") for i in range(2)]
    c_all = [sb(M0 + 27648 + i * 2048, [128, TB], F32, f"call{i}") for i in range(8)]
    csq = [sb(M0 + 44032 + i * 2048, [128, TB], F32, f"csq{i}") for i in range(2)]
    lnt = [sb(M0 + 48128 + i * 2048, [128, TB], F32, f"lnt{i}") for i in range(6)]
    assert M0 + 48128 + 6 * 2048 <= C0 + 102400
    mT = [sb(M0 + i * 1024, [128, TB], BF16, f"mT{i}") for i in range(16)]
    sa = [sb(M0 + 16384 + i * 2048, [128, TB], F32, f"sa{i}") for i in range(4)]
    sbt = [sb(M0 + 24576 + i * 2048, [128, TB], F32, f"sbt{i}") for i in range(4)]

    banks = []
    for i in range(8):
        h = nc.alloc_psum_tensor(f"pb{i}", [128, 512], F32)
        banks.append(Tile(h, [("P", i)], f"bank{i}"))
    bank_rr = [0]

    bank_reserved = set()

    def nb(n):
        r = []
        while len(r) < n:
            bk = banks[bank_rr[0] % 8]
            bank_rr[0] = (bank_rr[0] + 1) % 8
            if bk in bank_reserved:
                continue
            r.append(bk)
        return r

    sem_names = ["pe", "act", "dve", "pool", "r0", "r1", "r2", "r3", "x0", "x1",
                 "o0", "o1", "o2", "o3", "gb", "cst", "wg"]
    import contextlib
    with contextlib.ExitStack() as es:
        sems = {n: es.enter_context(nc.semaphore("s_" + n)) for n in sem_names}
        esems = {"pe": sems["pe"], "act": sems["act"], "dve": sems["dve"], "pool": sems["pool"]}
        ring_sems = [sems["r0"], sems["r1"], sems["r2"], sems["r3"]]
        xin_sems = [sems["x0"], sems["x1"]]
        out_sems = [sems["o0"], sems["o1"], sems["o2"], sems["o3"]]
        block = es.enter_context(nc.Block())

        flip = [0]

        def ev_engine():
            flip[0] ^= 1
            return "act" if flip[0] else "dve"

        def dma_in(q, dst_tile, dst_ap, src_ap, sem):
            S.add(q, lambda e: e.dma_start(out=dst_ap, in_=src_ap), reads=(), writes=(dst_tile,), dma_sem=sem)

        st_rr = [0]

        def st_tile():
            st_rr[0] = (st_rr[0] + 1) % NST
            return sts[st_rr[0]]

        ring_rr = [0]

        def load_w(src2d, r0, nk, c0, ncols=512):
            i = ring_rr[0]
            ring_rr[0] = (i + 1) % 4
            slot = ring[i]
            src = src2d[r0:r0 + nk * 128, c0:c0 + ncols].rearrange("(kc p) n -> p kc n", p=128)
            dst = slot.ap()[:, 0:nk, 0:ncols]
            S.add("pool", lambda e: e.dma_start(out=dst, in_=src), writes=(slot,), dma_sem=ring_sems[i])
            return slot

        def copy_scaled(dst_tile, dst_ap, src_tile, src_ap, scale_ap=None, eng=None, extra_reads=()):
            eng = eng or ev_engine()
            if eng == "act":
                if scale_ap is None:
                    f = lambda e: e.activation(out=dst_ap, in_=src_ap, func=AF.Copy)
                else:
                    f = lambda e: e.activation(out=dst_ap, in_=src_ap, func=AF.Copy, scale=scale_ap)
            else:
                if scale_ap is None:
                    f = lambda e: e.tensor_copy(out=dst_ap, in_=src_ap)
                else:
                    f = lambda e: e.tensor_scalar(out=dst_ap, in0=src_ap, scalar1=scale_ap, scalar2=None,
                                                  op0=ALU.mult)
            S.add(eng, f, reads=(src_tile,) + tuple(extra_reads), writes=(dst_tile,))

        def rstd_from_ss(ss_tile, ss_ap, ncols, inv_n, eps, rows=128):
            st = st_tile()
            a = st.ap()[0:rows, :]
            if ncols > 1:
                S.add("dve", lambda e: e.tensor_reduce(out=a[:, 0:1], in_=ss_ap, axis=AX.X, op=ALU.add),
                      reads=(ss_tile,), writes=(st,))
                S.add("dve", lambda e: e.tensor_scalar(out=a[:, 1:2], in0=a[:, 0:1], scalar1=inv_n, scalar2=eps,
                                                       op0=ALU.mult, op1=ALU.add), reads=(st,), writes=(st,))
            else:
                S.add("dve", lambda e: e.tensor_scalar(out=a[:, 1:2], in0=ss_ap, scalar1=inv_n, scalar2=eps,
                                                       op0=ALU.mult, op1=ALU.add), reads=(ss_tile,), writes=(st,))
            S.add("act", lambda e: e.activation(out=a[:, 2:3], in_=a[:, 1:2], func=AF.Sqrt),
                  reads=(st,), writes=(st,))
            S.add("dve", lambda e: e.reciprocal(out=a[:, 3:4], in_=a[:, 2:3]), reads=(st,), writes=(st,))
            return st, a[:, 3:4]

        def norm_transpose(src_tiles, nrows, gain, dst_fn):
            for ti, (src_t, src_ap) in enumerate(src_tiles):
                xsb = xs[ti % 2]
                xsa = xsb.ap()[0:nrows, :]
                st = st_tile()
                ssa = st.ap()[0:nrows, 0:1]
                S.add("act", lambda e, xsa=xsa, src_ap=src_ap, ssa=ssa: e.activation(
                    out=xsa, in_=src_ap, func=AF.Square, accum_out=ssa), reads=(src_t,), writes=(xsb, st))
                rt, ra = rstd_from_ss(st, ssa, 1, 1.0 / D, RMS_EPS, rows=nrows)
                S.add("dve", lambda e, xsa=xsa, src_ap=src_ap, ra=ra: e.tensor_scalar(
                    out=xsa, in0=src_ap, scalar1=ra, scalar2=None, op0=ALU.mult), reads=(src_t, rt), writes=(xsb,))
                for half in range(2):
                    bk = nb(1)[0]
                    bview = bk.ap().bitcast(BF16)

                    def tp(e, half=half, bview=bview, xsb=xsb):
                        inst = None
                        for i in range(8):
                            kc = half * 8 + i
                            inst = e.transpose(out=bview[:, i * nrows:(i + 1) * nrows],
                                               in_=xsb.ap()[0:nrows, kc * 128:(kc + 1) * 128],
                                               identity=ident_b.ap()[0:nrows, 0:nrows])
                        return inst
                    S.add("pe", tp, reads=(xsb, ident_b), writes=(bk,))
                    evq = ev_engine()
                    for i in range(8):
                        kc = half * 8 + i
                        dt_, da = dst_fn(kc, ti)
                        copy_scaled(dt_, da, bk, bview[:, i * nrows:(i + 1) * nrows],
                                    scale_ap=gain.ap()[:, kc:kc + 1], eng=evq, extra_reads=(gain,))

        csem = sems["cst"]
        n0 = len(S.ops)
        for t, d in ((ident_f, idn_d), (wdw, wdw_d), (gpm, gpm_d), (gpl, gpl_d), (psc, psc_d), (bdw, bdw_d),
                     (lng, lng_d), (lnb, lnb_d)):
            dma_in("sp", t, t.ap(), d, csem)
        for o in S.ops[n0:]:
            o.sigval = 16 * (len(S.ops) - n0)
        S.add("pool", lambda e: e.dma_start(out=wgrp.ap(),
                                            in_=w_grp_d.rearrange("g (kc p) d -> p g kc d", p=128)),
              writes=(wgrp,), dma_sem=sems["wg"])
        S.add("dve", lambda e: e.tensor_copy(out=ident_b.ap(), in_=ident_f.ap()), reads=(ident_f,), writes=(ident_b,))
        S.add("dve", lambda e: e.memset(ones_f.ap(), 1.0), writes=(ones_f,))
        S.add("dve", lambda e: e.memset(halo_a.ap(), 0.0), writes=(halo_a,))

        def proj_fm(wsrc, c0, kchunks, rhs_fn, ntok, bks, meta_bank=None):
            nkh = kchunks // 8
            slots = []
            for kh in range(nkh):
                slot = load_w(wsrc, kh * 1024, 8, c0)
                slots.append(slot)
                for oc in range(4):
                    def mm(e, kh=kh, oc=oc, slot=slot):
                        inst = None
                        for k in range(8):
                            kk = kh * 8 + k
                            rt, ra = rhs_fn(kk)
                            inst = e.matmul(bks[oc].ap()[:, 0:ntok], lhsT=slot.ap()[:, k, oc * 128:(oc + 1) * 128],
                                            rhs=ra, start=(kk == 0), stop=(kk == kchunks - 1))
                        return inst
                    rd = [slot] + [rhs_fn(kh * 8 + k)[0] for k in range(8)]
                    S.add("pe", mm, reads=rd, writes=(bks[oc],))
            if meta_bank is not None:
                def mmeta(e):
                    inst = None
                    for oc in range(4):
                        for kk in range(kchunks):
                            inst = e.matmul(meta_bank.ap()[:, oc * NMETA:(oc + 1) * NMETA],
                                            lhsT=slots[kk // 8].ap()[:, kk % 8, oc * 128:(oc + 1) * 128],
                                            rhs=uTm.ap()[:, kk, :], start=(kk == 0), stop=(kk == kchunks - 1))
                    return inst
                S.add("pe", mmeta, reads=slots + [uTm], writes=(meta_bank,))

        def glu_into(j, bank, ntok, a_tile, a_ap):
            s = sg[j % 2]
            S.add("act", lambda e: e.activation(out=s.ap()[:, 0:ntok], in_=bank.ap()[:, 0:ntok], func=AF.Sigmoid),
                  reads=(bank,), writes=(s,))
            S.add("dve", lambda e: e.tensor_tensor(out=a_ap, in0=c_all[j].ap()[:, 0:ntok], in1=s.ap()[:, 0:ntok],
                                                   op=ALU.mult), reads=(c_all[j], s), writes=(a_tile,))

        dma_in("sp", xin[0], xin[0].ap()[0:NMETA, :], meta_d, xin_sems[0])
        norm_transpose([(xin[0], xin[0].ap()[0:NMETA, :])], NMETA, gpm,
                       lambda kc, ti: (uTm, uTm.ap()[:, kc, :]))
        def nt_front(src_t, src_ap, tt, scale_eng="dve"):
            xsb = xs[tt % 2]
            st = st_tile()
            ssa = st.ap()[:, 0:1]
            S.add("act", lambda e: e.activation(out=xsb.ap(), in_=src_ap, func=AF.Square, accum_out=ssa),
                  reads=(src_t,), writes=(xsb, st))
            rt, ra = rstd_from_ss(st, st.ap()[:, 0:1], 1, 1.0 / D, RMS_EPS)
            if scale_eng == "act":
                S.add("act", lambda e: e.activation(out=xsb.ap(), in_=src_ap, func=AF.Copy, scale=ra),
                      reads=(src_t, rt), writes=(xsb,))
            else:
                S.add("dve", lambda e: e.tensor_scalar(out=xsb.ap(), in0=src_ap, scalar1=ra, scalar2=None, op0=ALU.mult),
                      reads=(src_t, rt), writes=(xsb,))

        def nt_back(tt, gain):
            xsb = xs[tt % 2]
            for half in range(2):
                bk = nb(1)[0]
                bview = bk.ap().bitcast(BF16)

                def tp(e, half=half, bview=bview):
                    inst = None
                    for i in range(8):
                        kc = half * 8 + i
                        inst = e.transpose(out=bview[:, i * 128:(i + 1) * 128],
                                           in_=xsb.ap()[:, kc * 128:(kc + 1) * 128], identity=ident_b.ap())
                    return inst
                S.add("pe", tp, reads=(xsb, ident_b), writes=(bk,))
                o_ap = uT_all.ap()[:, half * 8:(half + 1) * 8, tt * 128:(tt + 1) * 128]
                i0 = bview[:, 0:1024].rearrange("p (k n) -> p k n", k=8)
                i1 = gain.ap()[:, half * 8:(half + 1) * 8].unsqueeze(2).broadcast_to([128, 8, 128])
                S.add("dve", lambda e, o_ap=o_ap, i0=i0, i1=i1: e.tensor_tensor(out=o_ap, in0=i0, in1=i1, op=ALU.mult),
                      reads=(bk, gain), writes=tuple(uT[half * 8 + i] for i in range(8)))

        def phase_A_front(b, tts):
            for tt in tts:
                xi = xin[tt % 2]
                dma_in("sp", xi, xi.ap(), x_d[b * TB + tt * 128:b * TB + (tt + 1) * 128, :], xin_sems[tt % 2])
                nt_front(xi, xi.ap(), tt)

        def phase_A_back(tts):
            for tt in tts:
                nt_back(tt, gpm)

        for b in range(nblk):
            t0 = b * TB
            if b == 0:
                phase_A_front(0, (0, 1))
                phase_A_back((0, 1))
                phase_A_front(0, (2, 3))
                phase_A_back((2, 3))
            u_rhs = lambda kk: (uT[kk], uT[kk].ap())

            def g_front(j, bk):
                a = ax[j % 2]
                aa = a.ap()
                glu_into(j, bk, TB, a, aa[:, 30:542])
                S.add("dve", lambda e, aa=aa, j=j: e.tensor_copy(out=aa[:, 0:30], in_=halo_a.ap()[:, j, :]),
                      reads=(halo_a,), writes=(a,))
                S.add("dve", lambda e, aa=aa, j=j: e.tensor_copy(out=halo_a.ap()[:, j, :], in_=aa[:, 512:542]),
                      reads=(a,), writes=(halo_a,))
                d = dg[j % 2]
                S.add("dve", lambda e, d=d, j=j: e.tensor_tensor(
                    out=d.ap(), in0=ident_b.ap().unsqueeze(1).broadcast_to([128, KCONV, 128]),
                    in1=wdw.ap()[:, j * KCONV:(j + 1) * KCONV].unsqueeze(2).broadcast_to([128, KCONV, 128]),
                    op=ALU.mult), reads=(ident_b, wdw), writes=(d,))

            def conv_pe(j):
                a = ax[j % 2]
                d = dg[j % 2]
                bk = nb(1)[0]

                def cm(e, a=a, d=d, bk=bk):
                    inst = None
                    for k in range(KCONV):
                        inst = e.matmul(bk.ap(), lhsT=d.ap()[:, k, :], rhs=a.ap()[:, k:k + 512],
                                        start=(k == 0), stop=(k == KCONV - 1))
                    return inst
                S.add("pe", cm, reads=(a, d), writes=(bk,))
                S.add("act", lambda e, j=j, bk=bk: e.activation(out=c_all[j].ap(), in_=bk.ap(), func=AF.Identity,
                                                                bias=bdw.ap()[:, j:j + 1]),
                      reads=(bk, bdw), writes=(c_all[j],))

            def gate_proj(og, which):
                c_base, gdst = ((3072, gsa), (5120, gsb))[which]
                bg = nb(4)
                proj_fm(w_in_d, c_base + og * 512, 16, u_rhs, TB, bg)
                for oc in range(4):
                    jj = og * 4 + oc
                    S.add("act", lambda e, oc=oc, bg=bg, gt=gdst[jj]: e.activation(
                        out=gt.ap(), in_=bg[oc].ap(), func=AF.Sigmoid), reads=(bg[oc],), writes=(gdst[jj],))

            for cg in range(4):
                bks = nb(4)
                mbk = nb(1)[0] if b == 0 else None
                proj_fm(w_in_d, cg * 512, 16, u_rhs, TB, bks, meta_bank=mbk)
                for oc in range(4):
                    j = (cg % 2) * 4 + oc
                    bk = bks[oc]
                    if b == 0:
                        if cg < 2:
                            copy_scaled(halo_z, halo_z.ap()[:, j, :], mbk, mbk.ap()[:, oc * NMETA + 1:(oc + 1) * NMETA])
                        else:
                            copy_scaled(vm, vm.ap()[:, j, :], mbk, mbk.ap()[:, oc * NMETA:(oc + 1) * NMETA])
                    if cg < 2:
                        z = zx[j % 2]
                        za = z.ap()
                        copy_scaled(z, za[:, 15:527], bk, bk.ap())
                        S.add("dve", lambda e, za=za, j=j: e.tensor_copy(out=za[:, 0:15], in_=halo_z.ap()[:, j, :]),
                              reads=(halo_z,), writes=(z,))
                        wlog = j // 2 + 1
                        src_t = z
                        for s_ in range(wlog):
                            sh = 1 << s_
                            lo = (1 << (s_ + 1)) - 1
                            dst_t = pt[s_ % 2]
                            S.add("dve", lambda e, d=dst_t.ap(), s=src_t.ap(), lo=lo, sh=sh: e.tensor_tensor(
                                out=d[:, lo:527], in0=s[:, lo:527], in1=s[:, lo - sh:527 - sh], op=ALU.add),
                                reads=(src_t,), writes=(dst_t,))
                            src_t = dst_t
                        dd = dsb[j]
                        S.add("dve", lambda e, dd=dd, s=src_t.ap(), za=za, w=float(1 << wlog): e.scalar_tensor_tensor(
                            out=dd.ap(), in0=s[:, 15:527], scalar=1.0 / w, in1=za[:, 15:527],
                            op0=ALU.mult, op1=ALU.subtract), reads=(src_t, z), writes=(dd,))
                        S.add("dve", lambda e, za=za, j=j: e.tensor_copy(out=halo_z.ap()[:, j, :], in_=za[:, 512:527]),
                              reads=(z,), writes=(halo_z,))
                    else:
                        copy_scaled(c_all[j], c_all[j].ap(), bk, bk.ap(), eng="act")
            gq = [(0, 0), (0, 1), (1, 0), (1, 1)]
            for half in range(2):
                bks = nb(4)
                mbk = nb(1)[0] if b == 0 else None
                proj_fm(w_in_d, (4 + half) * 512, 16, u_rhs, TB, bks, meta_bank=mbk)
                j0 = half * 4
                if b == 0:
                    for oc in range(4):
                        j = j0 + oc
                        sm = sg[j % 2]
                        S.add("act", lambda e, sm=sm, oc=oc, mbk=mbk: e.activation(
                            out=sm.ap()[:, 0:NMETA], in_=mbk.ap()[:, oc * NMETA:(oc + 1) * NMETA], func=AF.Sigmoid),
                            reads=(mbk,), writes=(sm,))
                        S.add("dve", lambda e, sm=sm, j=j: e.tensor_tensor(
                            out=halo_a.ap()[:, j, 14:30], in0=vm.ap()[:, j, :], in1=sm.ap()[:, 0:NMETA], op=ALU.mult),
                            reads=(vm, sm), writes=(halo_a,))
                bank_reserved.update(bks)
                g_front(j0, bks[0])
                bank_reserved.discard(bks[0])
                g_front(j0 + 1, bks[1])
                bank_reserved.discard(bks[1])
                gate_proj(*gq[half * 2])
                conv_pe(j0)
                conv_pe(j0 + 1)
                g_front(j0 + 2, bks[2])
                bank_reserved.discard(bks[2])
                g_front(j0 + 3, bks[3])
                bank_reserved.discard(bks[3])
                gate_proj(*gq[half * 2 + 1])
                conv_pe(j0 + 2)
                conv_pe(j0 + 3)
            gate_proj(2, 0)
            for gi in range(4):
                for dj in range(2):
                    gb_ = nb(1)[0]

                    def gm(e, gi=gi, dj=dj, gb_=gb_):
                        inst = None
                        for kc in range(2):
                            inst = e.matmul(gb_.ap(), lhsT=wgrp.ap()[:, gi, kc, dj * 128:(dj + 1) * 128],
                                            rhs=dsb[2 * gi + kc].ap(), start=(kc == 0), stop=(kc == 1))
                        return inst
                    S.add("pe", gm, reads=(wgrp, dsb[2 * gi], dsb[2 * gi + 1]), writes=(gb_,))
                    jj = 2 * gi + dj
                    copy_scaled(pA[jj], pA[jj].ap(), gb_, gb_.ap(), scale_ap=psc.ap()[:, jj:jj + 1],
                                extra_reads=(psc,))
            b_sum, b_sq = nb(2)
            for j in range(8):
                q = csq[j % 2]
                S.add("act", lambda e, q=q, j=j: e.activation(out=q.ap(), in_=c_all[j].ap(), func=AF.Square),
                      reads=(c_all[j],), writes=(q,))

                def lm(e, q=q, j=j, b_sum=b_sum, b_sq=b_sq):
                    e.matmul(b_sum.ap(), lhsT=ones_f.ap(), rhs=c_all[j].ap(), start=(j == 0), stop=(j == 7))
                    return e.matmul(b_sq.ap(), lhsT=ones_f.ap(), rhs=q.ap(), start=(j == 0), stop=(j == 7))
                S.add("pe", lm, reads=(ones_f, c_all[j], q), writes=(b_sum, b_sq))
            mean, var, sd, rstd, nmr = lnt[0], lnt[1], lnt[2], lnt[3], lnt[4]
            S.add("dve", lambda e, b_sum=b_sum: e.tensor_scalar(out=mean.ap(), in0=b_sum.ap(), scalar1=1.0 / CW, scalar2=None,
                                                   op0=ALU.mult), reads=(b_sum,), writes=(mean,))
            S.add("dve", lambda e: e.tensor_tensor(out=var.ap(), in0=mean.ap(), in1=mean.ap(), op=ALU.mult),
                  reads=(mean,), writes=(var,))
            S.add("dve", lambda e, b_sq=b_sq: e.scalar_tensor_tensor(out=var.ap(), in0=b_sq.ap(), scalar=1.0 / CW, in1=var.ap(),
                                                          op0=ALU.mult, op1=ALU.subtract),
                  reads=(b_sq, var), writes=(var,))
            S.add("dve", lambda e: e.tensor_scalar(out=var.ap(), in0=var.ap(), scalar1=LN_EPS, scalar2=None,
                                                   op0=ALU.add), reads=(var,), writes=(var,))
            gate_proj(2, 1)
            gate_proj(3, 0)
            gate_proj(3, 1)
            S.add("act", lambda e: e.activation(out=sd.ap(), in_=var.ap(), func=AF.Sqrt), reads=(var,), writes=(sd,))
            S.add("dve", lambda e: e.reciprocal(out=rstd.ap(), in_=sd.ap()), reads=(sd,), writes=(rstd,))
            S.add("dve", lambda e: e.scalar_tensor_tensor(out=nmr.ap(), in0=mean.ap(), scalar=-1.0, in1=rstd.ap(),
                                                          op0=ALU.mult, op1=ALU.mult),
                  reads=(mean, rstd), writes=(nmr,))
            for j in range(8):
                tmp = csq[j % 2]
                S.add("dve", lambda e, tmp=tmp, j=j: e.tensor_tensor(out=tmp.ap(), in0=c_all[j].ap(), in1=rstd.ap(),
                                                                     op=ALU.mult),
                      reads=(c_all[j], rstd), writes=(tmp,))
                S.add("dve", lambda e, tmp=tmp: e.tensor_tensor(out=tmp.ap(), in0=tmp.ap(), in1=nmr.ap(), op=ALU.add),
                      reads=(tmp, nmr), writes=(tmp,))
                S.add("act", lambda e, tmp=tmp, j=j: e.activation(out=pB[j].ap(), in_=tmp.ap(), func=AF.Silu,
                                                                  scale=lng.ap()[:, j:j + 1], bias=lnb.ap()[:, j:j + 1]),
                      reads=(tmp, lng, lnb), writes=(pB[j],))

            for og in range(4):
                bya = nb(4)
                proj_fm(w_po_d, og * 512, 8, lambda kk: (pA[kk], pA[kk].ap()), TB, bya)
                for oc in range(4):
                    jj = og * 4 + oc
                    S.add("dve", lambda e, oc=oc, bya=bya, jj=jj: e.tensor_tensor(out=sa[oc].ap(), in0=bya[oc].ap(),
                                                                                 in1=gsa[jj].ap(), op=ALU.mult),
                          reads=(bya[oc], gsa[jj]), writes=(sa[oc],))
                byb = nb(4)
                proj_fm(w_co_d, og * 512, 8, lambda kk: (pB[kk], pB[kk].ap()), TB, byb)
                for oc in range(4):
                    jj = og * 4 + oc
                    S.add("dve", lambda e, oc=oc, byb=byb, jj=jj: e.tensor_tensor(out=sbt[oc].ap(), in0=byb[oc].ap(),
                                                                                 in1=gsb[jj].ap(), op=ALU.mult),
                          reads=(byb[oc], gsb[jj]), writes=(sbt[oc],))
                    S.add("dve", lambda e, oc=oc, jj=jj: e.tensor_tensor(out=mT[jj].ap(), in0=sbt[oc].ap(),
                                                                         in1=sa[oc].ap(), op=ALU.add),
                          reads=(sbt[oc], sa[oc]), writes=(mT[jj],))

            def proj_tm(wsrc, nkh, act_tiles, dst, gsrc_d, gsem, hook=None):
                dma_in("sp", gbuf, gbuf.ap(), gsrc_d.partition_broadcast(128), gsem)
                pss = [st_tile() for _ in range(4)]
                for og in range(4):
                    bks = nb(4)
                    for kh in range(nkh):
                        slot = load_w(wsrc, kh * 1024, 8, og * 512)
                        for tt in range(4):
                            def mm(e, kh=kh, tt=tt, slot=slot, bks=bks):
                                inst = None
                                for k in range(8):
                                    kk = kh * 8 + k
                                    inst = e.matmul(bks[tt].ap(), lhsT=act_tiles[kk].ap()[:, tt * 128:(tt + 1) * 128],
                                                    rhs=slot.ap()[:, k, :], start=(kk == 0), stop=(kk == nkh * 8 - 1))
                                return inst
                            S.add("pe", mm, reads=[slot] + [act_tiles[kh * 8 + k] for k in range(8)],
                                  writes=(bks[tt],))
                    if og == 0 and hook is not None:
                        bank_reserved.update(bks)
                        hook()
                        bank_reserved.clear()
                    for tt in range(4):
                        q = csq[tt % 2] if dst is hT else rs[tt % 2]
                        S.add("act", lambda e, tt=tt, og=og, q=q, bks=bks: e.activation(
                            out=q.ap(), in_=bks[tt].ap(), func=AF.Square,
                            accum_out=pss[tt].ap()[:, 4 + og:5 + og]),
                            reads=(bks[tt],), writes=(q, pss[tt]))
                        S.add("dve", lambda e, tt=tt, og=og, bks=bks: e.tensor_tensor(
                            out=dst[tt].ap()[:, og * 512:(og + 1) * 512], in0=bks[tt].ap(),
                            in1=gbuf.ap()[:, og * 512:(og + 1) * 512], op=ALU.mult),
                            reads=(bks[tt], gbuf), writes=(dst[tt],))
                return pss

            pss = proj_tm(w_o_d, 2, mT, hT, gqm_d, sems["gb"])
            for tt in range(4):
                rt, ra = rstd_from_ss(pss[tt], pss[tt].ap()[:, 4:8], 4, 1.0 / D, RMS_EPS)
                xi = xin[tt % 2]
                dma_in("sp", xi, xi.ap(), x_d[t0 + tt * 128:t0 + (tt + 1) * 128, :], xin_sems[tt % 2])
                h = hT[tt]
                S.add("dve", lambda e, h=h, ra=ra, xi=xi: e.scalar_tensor_tensor(out=h.ap(), in0=h.ap(), scalar=ra,
                                                                                 in1=xi.ap(), op0=ALU.mult, op1=ALU.add),
                      reads=(h, rt, xi), writes=(h,))
                if tt >= 1:
                    nt_back(tt - 1, gpl)
                nt_front(h, h.ap(), tt, scale_eng="act")
            nt_back(3, gpl)

            for cg in range(16):
                bks = nb(4)
                proj_fm(w_up_d, cg * 512, 16, u_rhs, TB, bks)
                for oc in range(4):
                    jj = cg * 4 + oc
                    r = rs[jj % 2]
                    S.add("act", lambda e, r=r, oc=oc, bks=bks: e.activation(out=r.ap(), in_=bks[oc].ap(), func=AF.Relu),
                          reads=(bks[oc],), writes=(r,))
                    S.add("dve", lambda e, r=r, jj=jj: e.tensor_tensor(out=hid[jj].ap(), in0=r.ap(), in1=r.ap(),
                                                                       op=ALU.mult),
                          reads=(r,), writes=(hid[jj],))

            hook = None
            if b + 1 < nblk:
                phase_A_front(b + 1, (0, 1))

                def hook(b=b):
                    phase_A_back((0, 1))
                    phase_A_front(b + 1, (2, 3))
                    phase_A_back((2, 3))

            pss = proj_tm(w_dn_d, 8, hid, fT, gql_d, sems["gb"], hook=hook)
            for tt in range(4):
                rt, ra = rstd_from_ss(pss[tt], pss[tt].ap()[:, 4:8], 4, 1.0 / D, RMS_EPS)
                f = fT[tt]
                S.add("dve", lambda e, f=f, ra=ra, tt=tt: e.scalar_tensor_tensor(out=f.ap(), in0=f.ap(), scalar=ra,
                                                                                 in1=hT[tt].ap(), op0=ALU.mult, op1=ALU.add),
                      reads=(f, rt, hT[tt]), writes=(f,))
                S.add("sp", lambda e, f=f, tt=tt, t0=t0: e.dma_start(
                    out=out_d[t0 + tt * 128:t0 + (tt + 1) * 128, :], in_=f.ap()),
                    reads=(f,), dma_sem=out_sems[tt])

        S.emit(block, esems, out_sems)
    return nc


_NC_CACHE = {}


def kernel(x, meta, g_pre_mix, w_in, w_pool_grp, pool_scale, w_pool_out, w_dw, b_dw, conv_ln_g, conv_ln_b,
           w_conv_out, w_o, g_post_mix, g_pre_mlp, w_up, w_down, g_post_mlp):
    f = lambda a: np.ascontiguousarray(np.asarray(a, dtype=np.float32))
    x = f(x)
    B = x.shape[0]

    def chan(v, n):
        return f(np.asarray(v, dtype=np.float32).reshape(n, 128).T)

    shared = {
        "meta": f(meta),
        "w_in": f(np.asarray(w_in)[0]),
        "w_grp": f(np.asarray(w_pool_grp)[0]),
        "w_po": f(np.asarray(w_pool_out)[0]),
        "w_co": f(np.asarray(w_conv_out)[0]),
        "w_o": f(np.asarray(w_o)[0]),
        "w_up": f(np.asarray(w_up)[0]),
        "w_dn": f(np.asarray(w_down)[0]),
        "gpm": chan(np.asarray(g_pre_mix)[0], 16),
        "gpl": chan(np.asarray(g_pre_mlp)[0], 16),
        "gqm": f(np.asarray(g_post_mix)[0].reshape(1, D)),
        "gql": f(np.asarray(g_post_mlp)[0].reshape(1, D)),
        "psc": chan(np.asarray(pool_scale)[0], 8),
        "wdw": f(np.asarray(w_dw, dtype=np.float32)[0].T.reshape(8, 128, KCONV).transpose(1, 0, 2).reshape(128, 8 * KCONV)),
        "bdw": chan(np.asarray(b_dw)[0], 8),
        "lng": chan(np.asarray(conv_ln_g)[0], 8),
        "lnb": chan(np.asarray(conv_ln_b)[0], 8),
        "idn": np.eye(128, dtype=np.float32),
    }
    if "nc" not in _NC_CACHE:
        _NC_CACHE["nc"] = build_program()
    nc = _NC_CACHE["nc"]
    in_maps = []
    for c in range(B):
        m = dict(shared)
        m["x"] = x[c]
        in_maps.append(m)
    res = run_bass_kernel_spmd(nc, in_maps, core_ids=list(range(B)))
    return np.stack([np.asarray(r["out"], dtype=np.float32) for r in res.results], axis=0)
```
